# Optimizing a Trainium2 kernel written in Bass

```python
import math
import jax, jax.numpy as jnp
from jax import lax
import numpy as np

D_MODEL = 2048
BATCH = 8
SEQ = 4096
DEPTH = 1

GRID_W = 64
CTX_LEN = 256
CONV_WIDTH = D_MODEL // 2
CONV_TAPS = 3
N_DIFF_HEADS = 8
DIFF_HEAD_DIM = 64
ATTN_WIDTH = N_DIFF_HEADS * 2 * DIFF_HEAD_DIM
ROPE_THETA = 10000.0
AXIS_ROT = DIFF_HEAD_DIM // 2
Q_BLOCK = 128
RMS_EPS = 1e-6
QK_SCALE = DIFF_HEAD_DIM ** -0.5

A_OFF = 0
Q_OFF = A_OFF + 4 * CONV_WIDTH
K_OFF = Q_OFF + ATTN_WIDTH
V_OFF = K_OFF + ATTN_WIDTH
ZB_OFF = V_OFF + ATTN_WIDTH
G_OFF = ZB_OFF + ATTN_WIDTH
PROJ_COLS = G_OFF + 2 * D_MODEL

kernel_name = "hybrid_shortconv_diffattn_dit_block"


def _rms(x):
    xf = x.astype(jnp.float32)
    return (xf * lax.rsqrt(jnp.mean(xf * xf, axis=-1, keepdims=True) + RMS_EPS)).astype(x.dtype)


def _axial_rope_tables(n_tokens, dtype):
    rows = n_tokens // GRID_W
    row = jnp.repeat(jnp.arange(rows, dtype=jnp.float32), GRID_W)
    col = jnp.tile(jnp.arange(GRID_W, dtype=jnp.float32), rows)
    inv_freq = ROPE_THETA ** (-jnp.arange(0, AXIS_ROT, 2, dtype=jnp.float32) / AXIS_ROT)
    ang_r = row[:, None] * inv_freq
    ang_c = col[:, None] * inv_freq
    ang = jnp.concatenate([ang_r, ang_r, ang_c, ang_c], axis=-1)
    return jnp.cos(ang).astype(dtype), jnp.sin(ang).astype(dtype)


def _rotate_half(y):
    y1, y2 = jnp.split(y, 2, axis=-1)
    return jnp.concatenate([-y2, y1], axis=-1)


def _apply_axial_rope(t, cos, sin):
    tr, tc = jnp.split(t, 2, axis=-1)
    rot = jnp.concatenate([_rotate_half(tr), _rotate_half(tc)], axis=-1)
    return t * cos[:, None, None, :] + rot * sin[:, None, None, :]


def _heads_qk(t):
    return t.reshape(t.shape[0], t.shape[1], N_DIFF_HEADS, 2, DIFF_HEAD_DIM)


def _heads_v(t):
    return t.reshape(t.shape[0], t.shape[1], N_DIFF_HEADS, 2 * DIFF_HEAD_DIM)


def _diff_softmax_mix(q, k, v, lam):
    s = jnp.einsum("bqhcd,bkhcd->bchqk", q, k).astype(jnp.float32)
    p = jax.nn.softmax(s, axis=-1)
    a = p[:, 0] - lam * p[:, 1]
    return jnp.einsum("bhqk,bkhe->bqhe", a.astype(v.dtype), v)


def _latent_diff_attention(q, k_lat, v_lat, k_ctx, v_ctx, lam):
    b, s = q.shape[0], q.shape[1]
    k = jnp.concatenate([k_lat, k_ctx], axis=1)
    v = jnp.concatenate([v_lat, v_ctx], axis=1)
    n_blk = s // Q_BLOCK
    q_blocks = q.reshape(b, n_blk, Q_BLOCK, N_DIFF_HEADS, 2, DIFF_HEAD_DIM).transpose(1, 0, 2, 3, 4, 5)
    out = lax.map(lambda qb: _diff_softmax_mix(qb, k, v, lam), q_blocks)
    return out.transpose(1, 0, 2, 3, 4).reshape(b, s, N_DIFF_HEADS, 2 * DIFF_HEAD_DIM)


def _short_conv_branch(p_a, conv_w, w_out_a):
    xa, gb, gc, za = jnp.split(p_a, 4, axis=-1)
    u = gc * xa
    up = jnp.pad(u, ((0, 0), (1, 1), (0, 0)))
    y = conv_w[0] * up[:, :-2] + conv_w[1] * up[:, 1:-1] + conv_w[2] * up[:, 2:]
    return (gb * y * jax.nn.silu(za)) @ w_out_a


def _diff_attn_out(o, zb, subln_w, lam_init, w_out_b):
    o = _rms(o) * subln_w * (1.0 - lam_init)
    o = o.reshape(o.shape[0], o.shape[1], ATTN_WIDTH)
    return (o * jax.nn.silu(zb)) @ w_out_b


def _merge(y_a, y_b, p_g, w_o):
    g_a, g_b = jnp.split(jax.nn.sigmoid(p_g), 2, axis=-1)
    return (g_a * y_a + g_b * y_b) @ w_o


def setup_inputs(seed: int = 0) -> dict:
    key = jax.random.key(seed)
    ks = jax.random.split(key, 14)
    f32 = jnp.float32
    x = jax.random.normal(ks[0], (BATCH, SEQ, D_MODEL), f32)
    c = jax.random.normal(ks[1], (BATCH, D_MODEL), f32)
    ctx = jax.random.normal(ks[2], (BATCH, CTX_LEN, D_MODEL), f32)
    c_ctx = jax.random.normal(ks[3], (D_MODEL,), f32)
    w_ada = jax.random.normal(ks[4], (DEPTH, D_MODEL, 3 * D_MODEL), f32) * (0.5 * D_MODEL ** -0.5)
    b_ada = jax.random.normal(ks[5], (DEPTH, 3 * D_MODEL), f32) * 0.01
    w_in = jax.random.normal(ks[6], (DEPTH, D_MODEL, PROJ_COLS), f32) * (D_MODEL ** -0.5)
    conv_w = jax.random.normal(ks[7], (DEPTH, CONV_TAPS, CONV_WIDTH), f32) * (CONV_TAPS ** -0.5)
    diff_lambda = jax.random.normal(ks[8], (DEPTH, 4, DIFF_HEAD_DIM), f32) * 0.1
    subln_w = 1.0 + 0.01 * jax.random.normal(ks[9], (DEPTH, 2 * DIFF_HEAD_DIM), f32)
    w_out_a = jax.random.normal(ks[10], (DEPTH, CONV_WIDTH, D_MODEL), f32) * (CONV_WIDTH ** -0.5)
    w_out_b = jax.random.normal(ks[11], (DEPTH, ATTN_WIDTH, D_MODEL), f32) * (ATTN_WIDTH ** -0.5)
    w_o = jax.random.normal(ks[12], (DEPTH, D_MODEL, D_MODEL), f32) * (D_MODEL ** -0.5)
    final_norm_w = 1.0 + 0.01 * jax.random.normal(ks[13], (D_MODEL,), f32)
    return {"x": x, "c": c, "ctx": ctx, "c_ctx": c_ctx, "w_ada": w_ada, "b_ada": b_ada,
            "w_in": w_in, "conv_w": conv_w, "diff_lambda": diff_lambda, "subln_w": subln_w,
            "w_out_a": w_out_a, "w_out_b": w_out_b, "w_o": w_o, "final_norm_w": final_norm_w}


def reference(x, c, ctx, c_ctx, w_ada, b_ada, w_in, conv_w, diff_lambda, subln_w,
              w_out_a, w_out_b, w_o, final_norm_w):
    cos, sin = _axial_rope_tables(x.shape[1], x.dtype)
    for l in range(DEPTH):
        lam_init = 0.8 - 0.6 * math.exp(-0.3 * l)
        dl = diff_lambda[l].astype(jnp.float32)
        lam = jnp.exp(jnp.sum(dl[0] * dl[1])) - jnp.exp(jnp.sum(dl[2] * dl[3])) + lam_init

        shift, scale, gate = jnp.split(jax.nn.silu(c) @ w_ada[l] + b_ada[l], 3, axis=-1)
        c_shift, c_scale, c_gate = jnp.split(jax.nn.silu(c_ctx) @ w_ada[l] + b_ada[l], 3, axis=-1)
        h = _rms(x) * (1.0 + scale[:, None]) + shift[:, None]
        hc = _rms(ctx) * (1.0 + c_scale) + c_shift

        if l + 1 < DEPTH:
            pc = hc @ w_in[l]
            q_c = _heads_qk(pc[..., Q_OFF:K_OFF]) * QK_SCALE
            k_c = _heads_qk(pc[..., K_OFF:V_OFF])
            v_c = _heads_v(pc[..., V_OFF:ZB_OFF])
            o_c = _diff_softmax_mix(q_c, k_c, v_c, lam)
            y_ac = _short_conv_branch(pc[..., A_OFF:Q_OFF], conv_w[l], w_out_a[l])
            y_bc = _diff_attn_out(o_c, pc[..., ZB_OFF:G_OFF], subln_w[l], lam_init, w_out_b[l])
            ctx_next = ctx + c_gate * _merge(y_ac, y_bc, pc[..., G_OFF:], w_o[l])
        else:
            pkv = hc @ w_in[l][:, K_OFF:ZB_OFF]
            k_c = _heads_qk(pkv[..., :ATTN_WIDTH])
            v_c = _heads_v(pkv[..., ATTN_WIDTH:])
            ctx_next = ctx

        p = h @ w_in[l]
        q = _apply_axial_rope(_heads_qk(p[..., Q_OFF:K_OFF]), cos, sin) * QK_SCALE
        k = _apply_axial_rope(_heads_qk(p[..., K_OFF:V_OFF]), cos, sin)
        v = _heads_v(p[..., V_OFF:ZB_OFF])
        o = _latent_diff_attention(q, k, v, k_c, v_c, lam)
        y_a = _short_conv_branch(p[..., A_OFF:Q_OFF], conv_w[l], w_out_a[l])
        y_b = _diff_attn_out(o, p[..., ZB_OFF:G_OFF], subln_w[l], lam_init, w_out_b[l])
        x = x + gate[:, None] * _merge(y_a, y_b, p[..., G_OFF:], w_o[l])
        ctx = ctx_next
    return _rms(x) * final_norm_w
```

```python
import math
from contextlib import ExitStack

import numpy as np
import concourse.bass as bass
import concourse.mybir as mybir
from concourse.bass_utils import run_bass_kernel_spmd

F32 = mybir.dt.float32
BF16 = mybir.dt.bfloat16
AF = mybir.ActivationFunctionType
ALU = mybir.AluOpType

D = 2048
import os
S = int(os.environ.get('KSIM_S', '4096'))
CTX = 256
KT_LEN = S + CTX
NT = S // 512
NXT = S // 128
NTT = KT_LEN // 128
NH = 8
KC = 16
A_OFF = 0
Q_OFF = 4096
K_OFF = 5120
V_OFF = 6144
ZB_OFF = 7168
G_OFF = 8192
LAM_INIT = 0.8 - 0.6 * math.exp(-0.3 * 0)
EPS = 1e-6
NCORES = 8
STOP = 99


class Buf:
    __slots__ = ("name", "w", "r", "dsem", "psum")

    def __init__(self, name, psum=False):
        self.name = name
        self.psum = psum
        self.w = {}
        self.r = {}
        self.dsem = None


class Eng:
    def __init__(self, name, h, sem):
        self.name = name
        self.h = h
        self.sem = sem
        self.count = 0
        self.waited = {}


class Sched:
    def __init__(self, nc, es):
        self.nc = nc
        self.es = es
        self.sems = {}
        self.nsem = 0
        mk = self.new_sem
        self.pe = Eng("pe", nc.tensor, mk("pe"))
        self.act = Eng("act", nc.scalar, mk("act"))
        self.dve = Eng("dve", nc.vector, mk("dve"))
        self.pool = Eng("pool", nc.gpsimd, mk("pool"))
        self.sp = Eng("sp", nc.sync, None)
        self.dma_bufs = []

    def new_sem(self, name):
        s = self.es.enter_context(self.nc.semaphore("s_" + name))
        self.nsem += 1
        key = self.nsem
        self.sems[key] = s
        return key

    def _wait(self, eng, deps):
        for k, v in deps.items():
            if eng is self.pe and k == self.pe.sem:
                continue
            if eng.waited.get(k, 0) >= v:
                continue
            eng.h.wait_ge(self.sems[k], v)
            eng.waited[k] = v

    @staticmethod
    def _merge(d, k, v):
        if d.get(k, 0) < v:
            d[k] = v

    def _deps(self, reads, writes, eng=None):
        deps = {}
        for b in reads:
            for k, v in b.w.items():
                self._merge(deps, k, v)
            if b.psum:
                for k, v in b.r.items():
                    if eng is None or k != eng.sem:
                        self._merge(deps, k, v)
        for b in writes:
            for k, v in b.w.items():
                self._merge(deps, k, v)
            for k, v in b.r.items():
                self._merge(deps, k, v)
        return deps

    def barrier(self):
        deps = {}
        for e in (self.pe, self.act, self.dve, self.pool):
            if e.count > 0:
                deps[e.sem] = e.count
        for b in self.dma_bufs:
            deps[b.dsem[0]] = b.dsem[1]
        for e in (self.pe, self.act, self.dve, self.pool, self.sp):
            d = {k: v for k, v in deps.items() if k != e.sem}
            for k, v in d.items():
                if e.waited.get(k, 0) >= v:
                    continue
                e.h.wait_ge(self.sems[k], v)
                e.waited[k] = v

    def op(self, eng, fn, reads=(), writes=(), sig=True, nowaw=False):
        deps = self._deps(reads, writes, eng)
        if nowaw:
            deps = self._deps(reads, [], eng)
            for b in writes:
                for k, v in b.w.items():
                    if k != eng.sem:
                        self._merge(deps, k, v)
                for k, v in b.r.items():
                    self._merge(deps, k, v)
        self._wait(eng, deps)
        inst = fn()
        if sig:
            inst.then_inc(self.sems[eng.sem], 1)
            eng.count += 1
            tv = eng.count
        else:
            tv = eng.count + 1
        for b in reads:
            self._merge(b.r, eng.sem, tv)
        for b in writes:
            if nowaw:
                b.w[eng.sem] = tv
            else:
                b.w = {eng.sem: tv}
            b.r = {}
        return inst

    def dma(self, qeng, out, in_, reads, wbuf):
        if wbuf.dsem is None:
            wbuf.dsem = [self.new_sem("d_" + wbuf.name), 0]
            self.dma_bufs.append(wbuf)
        self._wait(qeng, self._deps(reads, [wbuf]))
        inst = qeng.h.dma_start(out=out, in_=in_)
        wbuf.dsem[1] += 16
        inst.then_inc(self.sems[wbuf.dsem[0]], 16)
        tk, tv = wbuf.dsem[0], wbuf.dsem[1]
        for b in reads:
            self._merge(b.r, tk, tv)
        wbuf.w = {tk: tv}
        wbuf.r = {}
        return inst

    def dma_store(self, qeng, out, in_, rbuf):
        if rbuf.dsem is None:
            rbuf.dsem = [self.new_sem("d_" + rbuf.name), 0]
            self.dma_bufs.append(rbuf)
        deps = {}
        for k, v in rbuf.w.items():
            self._merge(deps, k, v)
        self._wait(qeng, deps)
        inst = qeng.h.dma_start(out=out, in_=in_)
        rbuf.dsem[1] += 16
        inst.then_inc(self.sems[rbuf.dsem[0]], 16)
        self._merge(rbuf.r, rbuf.dsem[0], rbuf.dsem[1])
        return inst

    def dma_more(self, qeng, out, in_, reads, wbuf):
        keep_w = dict(wbuf.w)
        deps = self._deps(reads, [])
        self._wait(qeng, deps)
        inst = qeng.h.dma_start(out=out, in_=in_)
        wbuf.dsem[1] += 16
        inst.then_inc(self.sems[wbuf.dsem[0]], 16)
        tk, tv = wbuf.dsem[0], wbuf.dsem[1]
        for b in reads:
            self._merge(b.r, tk, tv)
        keep_w[tk] = tv
        wbuf.w = keep_w
        return inst


def build_nc():
    nc = bass.Bass("TRN2", target_bir_lowering=False)

    def din(name, shape, dt=F32):
        return nc.dram_tensor(name, list(shape), dt, kind="ExternalInput").ap()

    x_d = din("x", [S, D])
    ctx_d = din("ctx", [CTX, D])
    cT_d = din("cT", [128, KC, 2])
    wada_d = din("w_ada", [D, 3 * D])
    bada_d = din("b_ada2", [2, 3 * D])
    win_d = din("w_in", [D, 12288])
    convw_d = din("conv_wT", [128, 8, 3])
    dlam_d = din("dlam", [128, 256])
    subln_d = din("sublnT", [128, 1])
    woa_d = din("w_out_a", [1024, D])
    wob_d = din("w_out_b", [1024, D])
    wo_d = din("w_o", [D, D])
    fnw_d = din("fnw_bc", [128, D])
    ident_d = din("ident", [128, 128])
    rotm_d = din("rotm", [128, 128])
    rope_d = din("ropeT", [2, 128, S])
    out_d = nc.dram_tensor("out", [S, D], F32, kind="ExternalOutput").ap()

    OB_d = nc.dram_tensor("scr_ob", [1024, S], BF16, kind="Internal").ap()
    YA_d = nc.dram_tensor("scr_ya", [1024, S], BF16, kind="Internal").ap()
    G_d = nc.dram_tensor("scr_g", [4096, S], BF16, kind="Internal").ap()
    M_d = nc.dram_tensor("scr_m", [D, S], BF16, kind="Internal").ap()
    GATE_d = nc.dram_tensor("scr_gate", [128, D], F32, kind="Internal").ap()

    with ExitStack() as top:
        blk = top.enter_context(nc.Block())

        @blk.sync
        def _(_sp):
            emit_program(nc, locals_=dict(
                x_d=x_d, ctx_d=ctx_d, cT_d=cT_d, wada_d=wada_d, bada_d=bada_d, win_d=win_d,
                convw_d=convw_d, dlam_d=dlam_d, subln_d=subln_d, woa_d=woa_d, wob_d=wob_d,
                wo_d=wo_d, fnw_d=fnw_d, ident_d=ident_d, rotm_d=rotm_d, rope_d=rope_d,
                out_d=out_d, OB_d=OB_d, YA_d=YA_d, G_d=G_d, M_d=M_d, GATE_d=GATE_d))
    return nc


def emit_program(nc, locals_):
    g = locals_
    x_d, ctx_d, cT_d, wada_d, bada_d, win_d = g["x_d"], g["ctx_d"], g["cT_d"], g["wada_d"], g["bada_d"], g["win_d"]
    convw_d, dlam_d, subln_d, woa_d, wob_d, wo_d = g["convw_d"], g["dlam_d"], g["subln_d"], g["woa_d"], g["wob_d"], g["wo_d"]
    fnw_d, ident_d, rotm_d, rope_d, out_d = g["fnw_d"], g["ident_d"], g["rotm_d"], g["rope_d"], g["out_d"]
    OB_d, YA_d, G_d, M_d, GATE_d = g["OB_d"], g["YA_d"], g["G_d"], g["M_d"], g["GATE_d"]

    with ExitStack() as es:
        sc = Sched(nc, es)
        PE, ACT, DVE, POOL, SP = sc.pe, sc.act, sc.dve, sc.pool, sc.sp
        T, V, A, P = nc.tensor, nc.vector, nc.scalar, nc.gpsimd

        def sb(stack, name, shape, dt):
            return stack.enter_context(nc.sbuf_tensor("sb_" + name, list(shape), dt))

        banks = [es.enter_context(nc.psum_tensor("pb%d" % i, [128, 512], F32)) for i in range(8)]
        bbuf = [Buf("pb%d" % i, psum=True) for i in range(8)]

        ident = sb(es, "ident", [128, 128], F32)
        rotm = sb(es, "rotm", [128, 128], BF16)
        modT = sb(es, "modT", [128, 48, 2], F32)
        sc1T = sb(es, "sc1T", [128, 16, 2], F32)
        epsT = sb(es, "epsT", [128, 1], F32)
        neglam = sb(es, "neglam", [128, 1], F32)
        sublnT = sb(es, "sublnT", [128, 1], F32)
        convw = sb(es, "convw", [128, 8, 3], F32)
        ones_row = sb(es, "ones_row", [1, 128], F32)
        b_ident, b_rotm, b_modT, b_sc1T, b_eps, b_neglam, b_subln, b_convw, b_ones = [
            Buf(n) for n in ("ident", "rotm", "modT", "sc1T", "eps", "neglam", "subln", "convw", "ones")]

        sc.dma(SP, ident[:], ident_d[:, :], [], b_ident)
        sc.dma(POOL, rotm[:], rotm_d[:, :], [], b_rotm)
        sc.dma(SP, sublnT[:], subln_d[:, :], [], b_subln)
        sc.dma(SP, convw[:], convw_d[:, :, :], [], b_convw)
        sc.op(POOL, lambda: P.memset(epsT[:], EPS), [], [b_eps])
        sc.op(POOL, lambda: P.memset(ones_row[:], 1.0), [], [b_ones])
        sc.op(DVE, lambda: V.tensor_scalar(out=sublnT[:], in0=sublnT[:], scalar1=float(1.0 - LAM_INIT), scalar2=None,
                                           op0=ALU.mult), [b_subln], [b_subln])

        if True:
            st = es
            dl = sb(st, "dl", [128, 256], F32)
            pr = sb(st, "dlpr", [128, 128], F32)
            junk = sb(st, "dljunk", [128, 64], F32)
            s12 = sb(st, "dls", [128, 2], F32)
            e12 = sb(st, "dle", [128, 2], F32)
            b_dl, b_pr, b_junk, b_s12, b_e12 = [Buf(n) for n in ("dl", "pr", "dljunk", "s12", "e12")]
            sc.dma(SP, dl[:], dlam_d[:, :], [], b_dl)
            sc.op(DVE, lambda: V.tensor_tensor(out=pr[:, 0:64], in0=dl[:, 0:64], in1=dl[:, 64:128], op=ALU.mult),
                  [b_dl], [b_pr])
            sc.op(DVE, lambda: V.tensor_tensor(out=pr[:, 64:128], in0=dl[:, 128:192], in1=dl[:, 192:256], op=ALU.mult),
                  [b_dl, b_pr], [b_pr])
            sc.op(ACT, lambda: A.activation(out=junk[:], in_=pr[:, 0:64], func=AF.Identity, accum_out=s12[:, 0:1]),
                  [b_pr], [b_junk, b_s12])
            sc.op(ACT, lambda: A.activation(out=junk[:], in_=pr[:, 64:128], func=AF.Identity, accum_out=s12[:, 1:2]),
                  [b_pr, b_s12], [b_junk, b_s12])
            sc.op(ACT, lambda: A.activation(out=e12[:], in_=s12[:], func=AF.Exp), [b_s12], [b_e12])
            sc.op(DVE, lambda: V.tensor_tensor(out=neglam[:], in0=e12[:, 1:2], in1=e12[:, 0:1], op=ALU.subtract),
                  [b_e12], [b_neglam])
            sc.op(DVE, lambda: V.tensor_scalar(out=neglam[:], in0=neglam[:], scalar1=float(-LAM_INIT), scalar2=None,
                                               op0=ALU.add), [b_neglam], [b_neglam])

        with ExitStack() as st:
            cT = sb(st, "cT", [128, KC, 2], F32)
            scT = sb(st, "scT", [128, KC, 128], F32)
            bada = sb(st, "bada", [2, 3 * D], F32)
            modsb = sb(st, "modsb", [2, 3 * D], F32)
            wa = [sb(st, "wa%d" % i, [128, KC, 512], F32) for i in range(2)]
            gbc = sb(st, "gbc", [128, D], F32)
            b_cT, b_scT, b_bada, b_modsb, b_gbc = [Buf(n) for n in ("cT", "scT", "bada", "modsb", "gbc")]
            b_wa = [Buf("wa0"), Buf("wa1")]
            sc.dma(SP, cT[:], cT_d[:, :, :], [], b_cT)
            sc.dma(SP, bada[:], bada_d[:, :], [], b_bada)
            sc.op(POOL, lambda: P.memset(scT[:], 0.0), [], [b_scT])
            sc.op(ACT, lambda: A.activation(out=scT[:, :, 0:2], in_=cT[:], func=AF.Silu), [b_cT], [b_scT])

            def load_wa(cb):
                w = wa[cb % 2]
                src = wada_d[:, cb * 512:(cb + 1) * 512].rearrange("(kc p) c -> p kc c", p=128)
                sc.dma(SP, w[:, 0:8, :], src[:, 0:8, :], [], b_wa[cb % 2])
                sc.dma_more(SP, w[:, 8:16, :], src[:, 8:16, :], [], b_wa[cb % 2])

            load_wa(0)
            for cb in range(12):
                if cb + 1 < 12:
                    load_wa(cb + 1)
                w = wa[cb % 2]
                bk = cb % 2
                for kc in range(KC):
                    sc.op(PE, lambda kc=kc: T.matmul(banks[bk][:, :], lhsT=scT[:, kc, :], rhs=w[:, kc, :],
                                                     start=(kc == 0), stop=(kc == KC - 1)),
                          [b_scT, b_wa[cb % 2]], [bbuf[bk]], sig=(kc == KC - 1))
                sc.op(DVE, lambda: V.tensor_tensor(out=modsb[:, cb * 512:(cb + 1) * 512], in0=banks[bk][0:2, :],
                                                   in1=bada[:, cb * 512:(cb + 1) * 512], op=ALU.add),
                      [bbuf[bk], b_bada], [b_modsb], nowaw=True)
            for j in range(48):
                sc.op(PE, lambda j=j: T.transpose(banks[2][:, 2 * j:2 * j + 2], modsb[:, j * 128:(j + 1) * 128],
                                                  ident[0:2, 0:2]),
                      [b_modsb, b_ident], [bbuf[2]], sig=(j == 47))
            sc.op(DVE, lambda: V.tensor_copy(out=modT[:].rearrange("p a b -> p (a b)"), in_=banks[2][:, 0:96]),
                  [bbuf[2]], [b_modT])
            sc.op(DVE, lambda: V.tensor_scalar(out=sc1T[:], in0=modT[:, 16:32, :], scalar1=1.0, scalar2=None,
                                               op0=ALU.add), [b_modT], [b_sc1T])
            b_gate_d = Buf("gate_d")
            for cb in range(4):
                bk = 3 + (cb % 2)
                sc.op(PE, lambda: T.matmul(banks[bk][:, :], lhsT=ones_row[0:1, :],
                                           rhs=modsb[0:1, 2 * D + cb * 512: 2 * D + (cb + 1) * 512],
                                           start=True, stop=True),
                      [b_ones, b_modsb], [bbuf[bk]])
                sc.op(ACT, lambda: A.copy(out=gbc[:, cb * 512:(cb + 1) * 512], in_=banks[bk][:, :]),
                      [bbuf[bk]], [b_gbc], nowaw=True)
            sc.dma_store(SP, GATE_d[:, :], gbc[:], b_gbc)

        sc.barrier()
        if STOP == 0:
            return
        hstack = ExitStack()
        hT = sb(hstack, "hT", [128, KC, KT_LEN], BF16)
        b_hT = [Buf("hT%d" % i) for i in range(NTT)]
        b_hTd = [Buf("hTd%d" % i) for i in range(NTT)]

        with ExitStack() as st:
            NXB = 3
            xt = [sb(st, "xt%d" % i, [128, D], F32) for i in range(NXB)]
            xn = [sb(st, "xn%d" % i, [128, D], F32) for i in range(2)]
            sqj = sb(st, "sqj", [128, D], BF16)
            ssq = [sb(st, "ssq%d" % i, [128, 1], F32) for i in range(2)]
            rst = [sb(st, "rst%d" % i, [128, 1], F32) for i in range(2)]
            b_xt = [Buf("xt%d" % i) for i in range(NXB)]
            b_xn = [Buf("xn%d" % i) for i in range(2)]
            b_sqj = Buf("sqj")
            b_ssq = [Buf("ssq0"), Buf("ssq1")]
            b_rst = [Buf("rst0"), Buf("rst1")]

            def load_x(tt):
                src = x_d[tt * 128:(tt + 1) * 128, :] if tt < NXT else ctx_d[(tt - NXT) * 128:(tt - NXT + 1) * 128, :]
                sc.dma(SP, xt[tt % NXB][:], src, [], b_xt[tt % NXB])

            load_x(0)
            load_x(1)
            for tt in range(NTT):
                if tt + 2 < NTT:
                    load_x(tt + 2)
                xi = tt % NXB
                i2 = tt % 2
                v = 0 if tt < NXT else 1
                sc.op(ACT, lambda: A.activation(out=sqj[:], in_=xt[xi][:], func=AF.Square, accum_out=ssq[i2][:]),
                      [b_xt[xi]], [b_sqj, b_ssq[i2]])
                sc.op(ACT, lambda: A.activation(out=rst[i2][:], in_=ssq[i2][:], func=AF.Sqrt, bias=epsT[:],
                                                scale=1.0 / D), [b_ssq[i2], b_eps], [b_rst[i2]])
                sc.op(DVE, lambda: V.reciprocal(out=rst[i2][:], in_=rst[i2][:]), [b_rst[i2]], [b_rst[i2]])
                sc.op(DVE, lambda: V.tensor_scalar(out=xn[i2][:], in0=xt[xi][:], scalar1=rst[i2][:], scalar2=None,
                                                   op0=ALU.mult), [b_xt[xi], b_rst[i2]], [b_xn[i2]])
                for gq in range(4):
                    bk = (tt * 4 + gq) % 8
                    for q in range(4):
                        kc = gq * 4 + q
                        sc.op(PE, lambda kc=kc, q=q: T.transpose(banks[bk][:, q * 128:(q + 1) * 128],
                                                                 xn[i2][:, kc * 128:(kc + 1) * 128], ident[:, :]),
                              [b_xn[i2], b_ident], [bbuf[bk]], sig=(q == 3))
                    for q in range(4):
                        kc = gq * 4 + q
                        dst = hT[:, kc, tt * 128:(tt + 1) * 128]
                        src = banks[bk][:, q * 128:(q + 1) * 128]
                        if gq % 2 == 0:
                            sc.op(ACT, lambda dst=dst, src=src, kc=kc: A.activation(
                                out=dst, in_=src, func=AF.Identity, scale=sc1T[:, kc, v:v + 1],
                                bias=modT[:, kc, v:v + 1]), [bbuf[bk], b_sc1T, b_modT], [b_hT[tt]], nowaw=True)
                        else:
                            sc.op(DVE, lambda dst=dst, src=src, kc=kc: V.tensor_scalar(
                                out=dst, in0=src, scalar1=sc1T[:, kc, v:v + 1], scalar2=modT[:, kc, v:v + 1],
                                op0=ALU.mult, op1=ALU.add), [bbuf[bk], b_sc1T, b_modT], [b_hTd[tt]], nowaw=True)

        sc.barrier()
        if STOP == 1:
            hstack.close()
            return

        def hT_bufs(tok0, n):
            return b_hT[tok0 // 128:(tok0 + n + 127) // 128] + b_hTd[tok0 // 128:(tok0 + n + 127) // 128]

        def load_w_cast(wt, wbuf, c0, ncols):
            src = win_d[:, c0:c0 + ncols].rearrange("(kc p) c -> p kc c", p=128)
            sc.dma(POOL, wt[:, 0:8, :], src[:, 0:8, :], [], wbuf)
            sc.dma_more(POOL, wt[:, 8:16, :], src[:, 8:16, :], [], wbuf)

        def proj_fm(bk, wt, wbuf, tok0, n, col0=0):
            hb = hT_bufs(tok0, n)
            for kc in range(KC):
                sc.op(PE, lambda kc=kc: T.matmul(banks[bk][:, 0:n], lhsT=wt[:, kc, col0:col0 + 128],
                                                 rhs=hT[:, kc, tok0:tok0 + n], start=(kc == 0), stop=(kc == KC - 1)),
                      [wbuf] + hb, [bbuf[bk]], sig=(kc == KC - 1))

        with ExitStack() as st:
            Wq, Wk, Wv, Wz = [sb(st, n, [128, KC, 128], BF16) for n in ("Wq", "Wk", "Wv", "Wz")]
            b_Wq, b_Wk, b_Wv, b_Wz = [Buf(n) for n in ("Wq", "Wk", "Wv", "Wz")]
            KTt = sb(st, "KTt", [128, KT_LEN], BF16)
            b_KT = [Buf("KT%d" % i) for i in range(NT + 1)]
            Vaug = sb(st, "Vaug", [128, NTT, 130], BF16)
            b_V = Buf("Vaug")
            b_Va = Buf("Vaug_a")
            QT = [sb(st, "QT%d" % i, [128, 512], BF16) for i in range(2)]
            b_QT = [Buf("QT0"), Buf("QT1")]
            ZS = [sb(st, "ZS%d" % i, [128, 512], BF16) for i in range(2)]
            b_ZS = [Buf("ZS0"), Buf("ZS1")]
            NE = 2
            Et = [sb(st, "E%d" % i, [128, 2, 512], BF16) for i in range(NE)]
            b_E = [[Buf("E%d_%d" % (i, c)) for c in range(2)] for i in range(NE)]
            cosT = [sb(st, "cos%d" % i, [128, 512], F32) for i in range(2)]
            sinT = [sb(st, "sin%d" % i, [128, 512], F32) for i in range(2)]
            b_cos = [Buf("cos0"), Buf("cos1")]
            b_sin = [Buf("sin0"), Buf("sin1")]
            qf = sb(st, "qf", [128, 512], F32)
            qb = sb(st, "qb", [128, 512], BF16)
            t1 = sb(st, "t1", [128, 512], F32)
            t2 = sb(st, "t2", [128, 512], F32)
            b_qf, b_qb, b_t1, b_t2 = [Buf(n) for n in ("qf", "qb", "t1", "t2")]
            rr = [sb(st, "rr%d" % i, [128, 4], F32) for i in range(2)]
            b_rr = [Buf("rr0"), Buf("rr1")]
            o1 = [sb(st, "o1_%d" % i, [128, 128], F32) for i in range(2)]
            oo = [sb(st, "oo_%d" % i, [128, 128], F32) for i in range(2)]
            on = [sb(st, "on_%d" % i, [128, 128], F32) for i in range(2)]
            ojk = sb(st, "ojk", [128, 128], BF16)
            b_o1 = [Buf("o1_0"), Buf("o1_1")]
            b_oo = [Buf("oo0"), Buf("oo1")]
            b_on = [Buf("on0"), Buf("on1")]
            b_ojk = Buf("ojk")
            obT = [sb(st, "obT%d" % i, [128, 512], BF16) for i in range(2)]
            b_obT = [Buf("obT0"), Buf("obT1")]
            b_OBd = [Buf("OBd%d" % t) for t in range(NT)]

            sc.op(POOL, lambda: P.memset(Vaug[:, :, 128:130], 1.0), [], [b_V])

            rope_ctr = [0]

            def rope_evac(bk, tt, dst, dbuf):
                ri = rope_ctr[0] % 2
                rope_ctr[0] += 1
                sc.dma(SP, cosT[ri][:], rope_d[0, :, tt * 512:(tt + 1) * 512], [], b_cos[ri])
                sc.dma(SP, sinT[ri][:], rope_d[1, :, tt * 512:(tt + 1) * 512], [], b_sin[ri])
                sc.op(ACT, lambda: A.copy(out=qf[:], in_=banks[bk][:, :]), [bbuf[bk]], [b_qf])
                sc.op(DVE, lambda: V.tensor_copy(out=qb[:], in_=banks[bk][:, :]), [bbuf[bk]], [b_qb])
                sc.op(PE, lambda: T.matmul(banks[bk][:, :], lhsT=rotm[:, :], rhs=qb[:, :], start=True, stop=True),
                      [b_rotm, b_qb], [bbuf[bk]])
                sc.op(POOL, lambda: P.tensor_tensor(out=t1[:], in0=qf[:], in1=cosT[ri][:], op=ALU.mult),
                      [b_qf, b_cos[ri]], [b_t1])
                sc.op(DVE, lambda: V.tensor_tensor(out=t2[:], in0=banks[bk][:, :], in1=sinT[ri][:], op=ALU.mult),
                      [bbuf[bk], b_sin[ri]], [b_t2])
                sc.op(POOL, lambda: P.tensor_tensor(out=dst, in0=t1[:], in1=t2[:], op=ALU.add),
                      [b_t1, b_t2], [dbuf])

            def q_zb_proj(h, t):
                i = t % 2
                proj_fm(7, Wq, b_Wq, t * 512, 512)
                rope_evac(7, t, QT[i][:, :], b_QT[i])
                proj_fm(7, Wz, b_Wz, t * 512, 512)
                sc.op(ACT, lambda: A.activation(out=ZS[i][:], in_=banks[7][:, :], func=AF.Silu),
                      [bbuf[7]], [b_ZS[i]])

            def acc_ap(c, qs, lo, hi):
                idx = c * 4 + qs
                bk = 4 + idx // 3
                off = (idx % 3) * 130
                return bk, banks[bk][:, off + lo:off + hi]

            for h in range(NH):
                load_w_cast(Wk, b_Wk, K_OFF + h * 128, 128)
                load_w_cast(Wv, b_Wv, V_OFF + h * 128, 128)
                load_w_cast(Wq, b_Wq, Q_OFF + h * 128, 128)
                load_w_cast(Wz, b_Wz, ZB_OFF + h * 128, 128)
                kbanks = [0, 1, 2, 3]
                for t in range(NT + 1):
                    bk = kbanks[t % 4]
                    n = 512 if t < NT else CTX
                    proj_fm(bk, Wk, b_Wk, t * 512, n)
                    if t < NT:
                        rope_evac(bk, t, KTt[:, t * 512:(t + 1) * 512], b_KT[t])
                    else:
                        sc.op(ACT, lambda: A.copy(out=KTt[:, S:KT_LEN], in_=banks[bk][:, 0:CTX]),
                              [bbuf[bk]], [b_KT[NT]])
                for g4 in range((NTT + 3) // 4):
                    bk = kbanks[g4 % 4]
                    nk = min(4, NTT - g4 * 4)
                    for i in range(nk):
                        kk = g4 * 4 + i
                        for kc in range(KC):
                            sc.op(PE, lambda kc=kc, kk=kk, i=i: T.matmul(
                                banks[bk][:, i * 128:(i + 1) * 128], lhsT=hT[:, kc, kk * 128:(kk + 1) * 128],
                                rhs=Wv[:, kc, :], start=(kc == 0), stop=(kc == KC - 1)),
                                [b_Wv, b_hT[kk], b_hTd[kk]], [bbuf[bk]], sig=(kc == KC - 1 and i == nk - 1))
                    sc.op(DVE if g4 % 2 == 0 else ACT,
                          (lambda: V.tensor_copy(out=Vaug[:, g4 * 4:g4 * 4 + nk, 0:128],
                                                 in_=banks[bk][:, 0:nk * 128].rearrange("p (a b) -> p a b", b=128)))
                          if g4 % 2 == 0 else
                          (lambda: A.copy(out=Vaug[:, g4 * 4:g4 * 4 + nk, 0:128],
                                          in_=banks[bk][:, 0:nk * 128].rearrange("p (a b) -> p a b", b=128))),
                          [bbuf[bk]], [b_V if g4 % 2 == 0 else b_Va], nowaw=True)
                q_zb_proj(h, 0)
                for t in range(NT):
                    qi = t % 2
                    q_t = QT[qi]

                    def qk(kk):
                        sb_ = (kk % 2) * 2
                        for c in range(2):
                            sc.op(PE, lambda c=c: T.matmul(
                                banks[sb_ + c][:, :], lhsT=KTt[c * 64:(c + 1) * 64, kk * 128:(kk + 1) * 128],
                                rhs=q_t[c * 64:(c + 1) * 64, :], start=True, stop=True),
                                [b_KT[kk // 4], b_QT[qi]], [bbuf[sb_ + c]])

                    qk(0)
                    for kk in range(NTT):
                        if kk + 1 < NTT:
                            qk(kk + 1)
                        sb_ = (kk % 2) * 2
                        ei = kk % NE
                        for c in range(2):
                            sc.op(ACT, lambda c=c: A.activation(out=Et[ei][:, c, :], in_=banks[sb_ + c][:, :],
                                                                func=AF.Exp, scale=0.125),
                                  [bbuf[sb_ + c]], [b_E[ei][c]])
                        for c in range(2):
                            for qs in range(4):
                                bk, ap = acc_ap(c, qs, 0, 129)
                                first_in_bank = ((c * 4 + qs) % 3 == 0)
                                sc.op(PE, lambda c=c, qs=qs, ap=ap, fb=first_in_bank: T.matmul(
                                    ap, lhsT=Et[ei][:, c, qs * 128:(qs + 1) * 128], rhs=Vaug[:, kk, 0:129],
                                    start=(kk == 0 and fb), stop=(kk == NTT - 1), skip_group_check=True),
                                    [b_E[ei][c], b_V, b_Va], [bbuf[bk]], sig=(qs == 3))
                    if t + 1 < NT:
                        q_zb_proj(h, t + 1)
                    for qs in range(4):
                        fi = qs % 2
                        bk1, s1 = acc_ap(0, qs, 128, 129)
                        bk2, s2 = acc_ap(1, qs, 128, 129)
                        _, a1 = acc_ap(0, qs, 0, 128)
                        _, a2 = acc_ap(1, qs, 0, 128)
                        r = rr[fi]
                        sc.op(DVE, lambda: V.reciprocal(out=r[:, 0:1], in_=s1), [bbuf[bk1]], [b_rr[fi]])
                        sc.op(DVE, lambda: V.reciprocal(out=r[:, 1:2], in_=s2), [bbuf[bk2], b_rr[fi]], [b_rr[fi]])
                        sc.op(DVE, lambda: V.tensor_tensor(out=r[:, 1:2], in0=r[:, 1:2], in1=neglam[:], op=ALU.mult),
                              [b_rr[fi], b_neglam], [b_rr[fi]])
                        sc.op(ACT, lambda: A.activation(out=o1[fi][:], in_=a1, func=AF.Identity, scale=r[:, 0:1]),
                              [bbuf[bk1], b_rr[fi]], [b_o1[fi]])
                        sc.op(DVE, lambda: V.scalar_tensor_tensor(out=oo[fi][:], in0=a2, scalar=r[:, 1:2],
                                                                  in1=o1[fi][:], op0=ALU.mult, op1=ALU.add),
                              [bbuf[bk2], b_rr[fi], b_o1[fi]], [b_oo[fi]])
                        sc.op(ACT, lambda: A.activation(out=ojk[:], in_=oo[fi][:], func=AF.Square,
                                                        accum_out=r[:, 2:3]),
                              [b_oo[fi], b_rr[fi]], [b_ojk, b_rr[fi]])
                        sc.op(ACT, lambda: A.activation(out=r[:, 3:4], in_=r[:, 2:3], func=AF.Sqrt, bias=epsT[:],
                                                        scale=1.0 / 128.0), [b_rr[fi], b_eps], [b_rr[fi]])
                        sc.op(DVE, lambda: V.reciprocal(out=r[:, 3:4], in_=r[:, 3:4]), [b_rr[fi]], [b_rr[fi]])
                        sc.op(DVE, lambda: V.tensor_scalar(out=on[fi][:], in0=oo[fi][:], scalar1=r[:, 3:4],
                                                           scalar2=None, op0=ALU.mult),
                              [b_oo[fi], b_rr[fi]], [b_on[fi]])
                        sc.op(PE, lambda: T.transpose(banks[7][:, qs * 128:(qs + 1) * 128], on[fi][:, :],
                                                      ident[:, :]),
                              [b_on[fi], b_ident], [bbuf[7]])
                    oi = t % 2
                    sc.op(DVE, lambda: V.scalar_tensor_tensor(out=obT[oi][:], in0=banks[7][:, :], scalar=sublnT[:],
                                                              in1=ZS[qi][:], op0=ALU.mult, op1=ALU.mult),
                          [bbuf[7], b_subln, b_ZS[qi]], [b_obT[oi]])
                    sc.dma_store(SP, OB_d[h * 128:(h + 1) * 128, t * 512:(t + 1) * 512], obT[oi][:], b_obT[oi])

        sc.barrier()
        if STOP == 2:
            hstack.close()
            return
        b_YAd = [Buf("YAd%d" % t) for t in range(NT)]
        with ExitStack() as st:
            NW = 2
            Wc = [[sb(st, "Wc%d_%d" % (i, k), [128, KC, 128], BF16) for k in range(4)] for i in range(NW)]
            b_Wc = [[Buf("Wc%d_%d" % (i, k)) for k in range(4)] for i in range(NW)]
            ufull = sb(st, "ufull", [128, S + 2], F32)
            b_u = [Buf("u%d" % t) for t in range(NT)]
            b_uh = Buf("uhalo")
            xas = sb(st, "xas", [128, 512], F32)
            szs = sb(st, "szs", [128, 512], F32)
            b_xas, b_szs = Buf("xas"), Buf("szs")
            g2 = [sb(st, "g2_%d" % i, [128, 512], BF16) for i in range(3)]
            b_g2 = [Buf("g2_%d" % i) for i in range(3)]
            yy = sb(st, "yy", [128, 512], F32)
            b_yy = Buf("yy")
            yat = [sb(st, "yat%d" % i, [128, 512], BF16) for i in range(2)]
            b_yat = [Buf("yat0"), Buf("yat1")]
            sc.op(POOL, lambda: P.memset(ufull[:, 0:1], 0.0), [], [b_uh])
            sc.op(POOL, lambda: P.memset(ufull[:, S + 1:S + 2], 0.0), [b_uh], [b_uh])

            def load_conv_w(j):
                i = j % NW
                for k in range(4):
                    load_w_cast(Wc[i][k], b_Wc[i][k], A_OFF + k * 1024 + j * 128, 128)

            cctr = [0]

            def conv_tile(j, t):
                ci = cctr[0]
                cctr[0] += 1
                rb = [b_u[max(t - 1, 0)], b_u[t], b_u[min(t + 1, NT - 1)], b_uh, b_convw]
                lo = t * 512
                sc.op(DVE, lambda: V.tensor_scalar(out=yy[:], in0=ufull[:, lo:lo + 512], scalar1=convw[:, j, 0:1],
                                                   scalar2=None, op0=ALU.mult), rb, [b_yy])
                sc.op(DVE, lambda: V.scalar_tensor_tensor(out=yy[:], in0=ufull[:, lo + 1:lo + 513],
                                                          scalar=convw[:, j, 1:2], in1=yy[:], op0=ALU.mult,
                                                          op1=ALU.add), rb + [b_yy], [b_yy])
                sc.op(DVE, lambda: V.scalar_tensor_tensor(out=yy[:], in0=ufull[:, lo + 2:lo + 514],
                                                          scalar=convw[:, j, 2:3], in1=yy[:], op0=ALU.mult,
                                                          op1=ALU.add), rb + [b_yy], [b_yy])
                yi = ci % 2
                gi = (j * NT + t) % 3
                sc.op(POOL, lambda: P.tensor_tensor(out=yat[yi][:], in0=yy[:], in1=g2[gi][:], op=ALU.mult),
                      [b_yy, b_g2[gi]], [b_yat[yi]])
                dst = YA_d[j * 128:(j + 1) * 128, lo:lo + 512]
                sc.dma_store(SP, dst, yat[yi][:], b_yat[yi])

            load_conv_w(0)
            for j in range(8):
                if j + 1 < 8:
                    load_conv_w(j + 1)
                wi = j % NW
                for t in range(NT):
                    bks = [(t % 2) * 4 + k for k in range(4)]
                    for k in range(4):
                        proj_fm(bks[k], Wc[wi][k], b_Wc[wi][k], t * 512, 512)
                    gi = (j * NT + t) % 3
                    sc.op(ACT, lambda: A.copy(out=xas[:], in_=banks[bks[0]][:, :]), [bbuf[bks[0]]], [b_xas])
                    sc.op(DVE, lambda: V.tensor_tensor(out=ufull[:, 1 + t * 512:1 + (t + 1) * 512],
                                                       in0=banks[bks[2]][:, :], in1=xas[:], op=ALU.mult),
                          [bbuf[bks[2]], b_xas], [b_u[t]])
                    sc.op(ACT, lambda: A.activation(out=szs[:], in_=banks[bks[3]][:, :], func=AF.Silu),
                          [bbuf[bks[3]]], [b_szs])
                    sc.op(DVE, lambda: V.tensor_tensor(out=g2[gi][:], in0=banks[bks[1]][:, :], in1=szs[:],
                                                       op=ALU.mult), [bbuf[bks[1]], b_szs], [b_g2[gi]])
                    if t >= 1:
                        conv_tile(j, t - 1)
                conv_tile(j, NT - 1)

        sc.barrier()
        if STOP == 3:
            hstack.close()
            return
        b_Gd = [Buf("Gd%d" % t) for t in range(NT)]
        with ExitStack() as st:
            Wg = [sb(st, "Wg%d" % i, [128, KC, 512], BF16) for i in range(2)]
            b_Wg = [Buf("Wg0"), Buf("Wg1")]
            gt = [sb(st, "gt%d" % i, [128, 512], BF16) for i in range(3)]
            b_gt = [Buf("gt%d" % i) for i in range(3)]
            load_w_cast(Wg[0], b_Wg[0], G_OFF, 512)
            ctr = 0
            for gb_ in range(8):
                if gb_ + 1 < 8:
                    load_w_cast(Wg[(gb_ + 1) % 2], b_Wg[(gb_ + 1) % 2], G_OFF + (gb_ + 1) * 512, 512)
                wi = gb_ % 2
                for sub in range(4):
                    gidx = gb_ * 4 + sub
                    for t in range(NT):
                        bk = ctr % 8
                        gi = ctr % 3
                        ctr += 1
                        proj_fm(bk, Wg[wi], b_Wg[wi], t * 512, 512, col0=sub * 128)
                        sc.op(ACT, lambda: A.activation(out=gt[gi][:], in_=banks[bk][:, :], func=AF.Sigmoid),
                              [bbuf[bk]], [b_gt[gi]])
                        dst = G_d[gidx * 128:(gidx + 1) * 128, t * 512:(t + 1) * 512]
                        sc.dma_store(SP, dst, gt[gi][:], b_gt[gi])

        sc.barrier()
        hstack.close()
        if STOP == 4:
            return

        b_Md = [Buf("Md%d" % t) for t in range(NT)]
        with ExitStack() as st:
            WoA = sb(st, "WoA", [128, 8, D], BF16)
            WoB = sb(st, "WoB", [128, 8, D], BF16)
            b_WoA, b_WoB = Buf("WoA"), Buf("WoB")
            for (wt, wb, src_d) in ((WoA, b_WoA, woa_d), (WoB, b_WoB, wob_d)):
                src = src_d.rearrange("(j p) c -> p j c", p=128)
                first = True
                for hh in range(2):
                    for jj in range(2):
                        (sc.dma if first else sc.dma_more)(
                            POOL, wt[:, jj * 4:(jj + 1) * 4, hh * 1024:(hh + 1) * 1024],
                            src[:, jj * 4:(jj + 1) * 4, hh * 1024:(hh + 1) * 1024], [], wb)
                        first = False
            yaT = [sb(st, "yaT%d" % i, [128, 8, 512], BF16) for i in range(2)]
            obT2 = [sb(st, "obT2_%d" % i, [128, 8, 512], BF16) for i in range(2)]
            gT = [sb(st, "gT%d" % i, [128, 32, 512], BF16) for i in range(2)]
            b_yaT = [Buf("yaT0"), Buf("yaT1")]
            b_obT2 = [Buf("obT2_0"), Buf("obT2_1")]
            b_gT = [Buf("gT0"), Buf("gT1")]
            mT = [sb(st, "mT%d" % i, [128, 16, 512], BF16) for i in range(2)]
            b_mT = [Buf("mT0"), Buf("mT1")]
            ta = [sb(st, "ta%d" % i, [128, 512], F32) for i in range(2)]
            tb = [sb(st, "tb%d" % i, [128, 512], F32) for i in range(2)]
            b_ta = [Buf("ta0"), Buf("ta1")]
            b_tb = [Buf("tb0"), Buf("tb1")]

            def load_2a(t):
                i = t % 2
                sc.dma(SP, yaT[i][:], YA_d[:, t * 512:(t + 1) * 512].rearrange("(j p) t -> p j t", p=128),
                       [], b_yaT[i])
                sc.dma(SP, obT2[i][:], OB_d[:, t * 512:(t + 1) * 512].rearrange("(j p) t -> p j t", p=128),
                       [], b_obT2[i])
                gsrc = G_d[:, t * 512:(t + 1) * 512].rearrange("(j p) t -> p j t", p=128)
                for q in range(4):
                    (sc.dma if q == 0 else sc.dma_more)(SP, gT[i][:, q * 8:(q + 1) * 8, :],
                                                        gsrc[:, q * 8:(q + 1) * 8, :], [], b_gT[i])

            load_2a(0)
            ctr = 0
            for t in range(NT):
                if t + 1 < NT:
                    load_2a(t + 1)
                i = t % 2
                for n in range(16):
                    bka = (ctr * 2) % 8
                    bkb = bka + 1
                    ti = ctr % 2
                    ctr += 1
                    for (bk, wt, wb, src, sbf) in ((bka, WoA, b_WoA, yaT[i], b_yaT[i]),
                                                   (bkb, WoB, b_WoB, obT2[i], b_obT2[i])):
                        for j in range(8):
                            sc.op(PE, lambda j=j, bk=bk, wt=wt, src=src: T.matmul(
                                banks[bk][:, :], lhsT=wt[:, j, n * 128:(n + 1) * 128], rhs=src[:, j, :],
                                start=(j == 0), stop=(j == 7)), [wb, sbf], [bbuf[bk]], sig=(j == 7))
                    sc.op(DVE, lambda: V.tensor_tensor(out=ta[ti][:], in0=banks[bka][:, :], in1=gT[i][:, n, :],
                                                       op=ALU.mult), [bbuf[bka], b_gT[i]], [b_ta[ti]])
                    sc.op(DVE, lambda: V.tensor_tensor(out=tb[ti][:], in0=banks[bkb][:, :], in1=gT[i][:, 16 + n, :],
                                                       op=ALU.mult), [bbuf[bkb], b_gT[i]], [b_tb[ti]])
                    sc.op(POOL, lambda: P.tensor_tensor(out=mT[i][:, n, :], in0=ta[ti][:], in1=tb[ti][:],
                                                        op=ALU.add), [b_ta[ti], b_tb[ti]], [b_mT[i]], nowaw=True)
                mdst = M_d[:, t * 512:(t + 1) * 512].rearrange("(j p) t -> p j t", p=128)
                sc.dma_store(SP, mdst[:, 0:8, :], mT[i][:, 0:8, :], b_mT[i])
                sc.dma_store(SP, mdst[:, 8:16, :], mT[i][:, 8:16, :], b_mT[i])

        sc.barrier()
        if STOP == 5:
            return
        b_out = Buf("out_d")
        with ExitStack() as st:
            Wo = sb(st, "Wo", [128, KC, D], BF16)
            b_Wo = Buf("Wo")
            wsrc = wo_d.rearrange("(kc p) c -> p kc c", p=128)
            first = True
            for hh in range(2):
                for jj in range(4):
                    (sc.dma if first else sc.dma_more)(
                        POOL, Wo[:, jj * 4:(jj + 1) * 4, hh * 1024:(hh + 1) * 1024],
                        wsrc[:, jj * 4:(jj + 1) * 4, hh * 1024:(hh + 1) * 1024], [], b_Wo)
                    first = False
            gbc2 = sb(st, "gbc2", [128, D], F32)
            fnw = sb(st, "fnw", [128, D], F32)
            b_gbc2, b_fnw = Buf("gbc2"), Buf("fnw")
            sc.dma(SP, gbc2[:], GATE_d[:, :], [], b_gbc2)
            sc.dma(SP, fnw[:], fnw_d[:, :], [], b_fnw)
            mT2 = [sb(st, "mT2_%d" % i, [128, KC, 512], BF16) for i in range(2)]
            b_mT2 = [Buf("mT2_0"), Buf("mT2_1")]
            xr = [sb(st, "xr%d" % i, [128, D], F32) for i in range(2)]
            b_xr = [Buf("xr0"), Buf("xr1")]
            zt = [sb(st, "zt%d" % i, [128, D], F32) for i in range(2)]
            b_zt = [Buf("zt0"), Buf("zt1")]
            tg = [sb(st, "tg%d" % i, [128, 512], F32) for i in range(2)]
            b_tg = [Buf("tg0"), Buf("tg1")]
            ot = [sb(st, "ot%d" % i, [128, D], F32) for i in range(2)]
            b_ot = [Buf("ot0"), Buf("ot1")]
            zj = sb(st, "zj", [128, D], BF16)
            b_zj = Buf("zj")
            fr = [sb(st, "fr%d" % i, [128, 2], F32) for i in range(2)]
            b_fr = [Buf("fr0"), Buf("fr1")]

            def load_m(t):
                i = t % 2
                msrc = M_d[:, t * 512:(t + 1) * 512].rearrange("(j p) t -> p j t", p=128)
                sc.dma(SP, mT2[i][:, 0:8, :], msrc[:, 0:8, :], [], b_mT2[i])
                sc.dma_more(SP, mT2[i][:, 8:16, :], msrc[:, 8:16, :], [], b_mT2[i])

            def load_xr(tt):
                sc.dma(SP, xr[tt % 2][:], x_d[tt * 128:(tt + 1) * 128, :], [], b_xr[tt % 2])

            load_m(0)
            load_xr(0)
            ctr = 0
            for t in range(NT):
                if t + 1 < NT:
                    load_m(t + 1)
                i = t % 2
                for qs in range(4):
                    tt = t * 4 + qs
                    if tt + 1 < NXT:
                        load_xr(tt + 1)
                    xi = tt % 2
                    for cb in range(4):
                        bk = ctr % 8
                        gi = ctr % 2
                        ctr += 1
                        for kc in range(KC):
                            sc.op(PE, lambda kc=kc, bk=bk: T.matmul(
                                banks[bk][:, :], lhsT=mT2[i][:, kc, qs * 128:(qs + 1) * 128],
                                rhs=Wo[:, kc, cb * 512:(cb + 1) * 512], start=(kc == 0), stop=(kc == KC - 1)),
                                [b_mT2[i], b_Wo], [bbuf[bk]], sig=(kc == KC - 1))
                        sc.op(DVE, lambda: V.tensor_tensor(out=tg[gi][:], in0=banks[bk][:, :],
                                                           in1=gbc2[:, cb * 512:(cb + 1) * 512], op=ALU.mult),
                              [bbuf[bk], b_gbc2], [b_tg[gi]])
                        sc.op(POOL, lambda: P.tensor_tensor(out=zt[xi][:, cb * 512:(cb + 1) * 512], in0=tg[gi][:],
                                                            in1=xr[xi][:, cb * 512:(cb + 1) * 512], op=ALU.add),
                              [b_tg[gi], b_xr[xi]], [b_zt[xi]], nowaw=True)
                    f = fr[xi]
                    sc.op(ACT, lambda: A.activation(out=zj[:], in_=zt[xi][:], func=AF.Square, accum_out=f[:, 0:1]),
                          [b_zt[xi]], [b_zj, b_fr[xi]])
                    sc.op(ACT, lambda: A.activation(out=f[:, 1:2], in_=f[:, 0:1], func=AF.Sqrt, bias=epsT[:],
                                                    scale=1.0 / D), [b_fr[xi], b_eps], [b_fr[xi]])
                    sc.op(DVE, lambda: V.reciprocal(out=f[:, 1:2], in_=f[:, 1:2]), [b_fr[xi]], [b_fr[xi]])
                    sc.op(DVE, lambda: V.scalar_tensor_tensor(out=ot[xi][:], in0=zt[xi][:], scalar=f[:, 1:2],
                                                              in1=fnw[:], op0=ALU.mult, op1=ALU.mult),
                          [b_zt[xi], b_fr[xi], b_fnw], [b_ot[xi]])
                    sc.dma_store(SP, out_d[tt * 128:(tt + 1) * 128, :], ot[xi][:], b_ot[xi])
            sc.barrier()
            print("semaphores used:", sc.nsem)


def _rope_tables():
    rows = S // 64
    row = np.repeat(np.arange(rows, dtype=np.float32), 64)
    col = np.tile(np.arange(64, dtype=np.float32), rows)
    inv_freq = (10000.0 ** (-np.arange(0, 32, 2, dtype=np.float32) / 32.0)).astype(np.float32)
    ang_r = row[:, None] * inv_freq
    ang_c = col[:, None] * inv_freq
    ang = np.concatenate([ang_r, ang_r, ang_c, ang_c], axis=-1).astype(np.float32)
    cosT = np.cos(ang).astype(np.float32).T
    sinT = np.sin(ang).astype(np.float32).T
    tab = np.stack([np.concatenate([cosT, cosT], 0), np.concatenate([sinT, sinT], 0)], 0)
    return np.ascontiguousarray(tab.astype(np.float32))


def _rot_matrix():
    R = np.zeros((128, 128), np.float32)
    for j in range(128):
        if (j % 32) < 16:
            R[j + 16, j] = -1.0
        else:
            R[j - 16, j] = 1.0
    return R


_NC_CACHE = {}


def kernel(x, c, ctx, c_ctx, w_ada, b_ada, w_in, conv_w, diff_lambda, subln_w,
           w_out_a, w_out_b, w_o, final_norm_w):
    f = lambda a: np.ascontiguousarray(np.asarray(a, dtype=np.float32))
    x, c, ctx, c_ctx = f(x), f(c), f(ctx), f(c_ctx)
    if "nc" not in _NC_CACHE:
        _NC_CACHE["nc"] = build_nc()
    nc = _NC_CACHE["nc"]
    shared = {
        "w_ada": f(w_ada[0]),
        "b_ada2": f(np.broadcast_to(np.asarray(b_ada[0])[None, :], (2, 3 * D))),
        "w_in": f(w_in[0]),
        "conv_wT": f(np.asarray(conv_w[0]).reshape(3, 8, 128).transpose(2, 1, 0)),
        "dlam": f(np.broadcast_to(np.asarray(diff_lambda[0]).reshape(1, 256), (128, 256))),
        "sublnT": f(np.asarray(subln_w[0]).reshape(128, 1)),
        "w_out_a": f(w_out_a[0]),
        "w_out_b": f(w_out_b[0]),
        "w_o": f(w_o[0]),
        "fnw_bc": f(np.broadcast_to(np.asarray(final_norm_w)[None, :], (128, D))),
        "ident": np.eye(128, dtype=np.float32),
        "rotm": _rot_matrix(),
        "ropeT": _rope_tables(),
    }
    in_maps = []
    for b in range(NCORES):
        cT = np.stack([c[b].reshape(KC, 128).T, c_ctx.reshape(KC, 128).T], axis=-1)
        m = dict(shared)
        m["x"] = x[b]
        m["ctx"] = ctx[b]
        m["cT"] = f(cT)
        in_maps.append(m)
    res = run_bass_kernel_spmd(nc, in_maps, core_ids=list(range(NCORES)))
    return np.stack([np.asarray(r["out"], dtype=np.float32) for r in res.results], axis=0)
```
